# Optimizing a Trainium2 kernel written in Bass

```python
import jax, jax.numpy as jnp
from jax import lax
import numpy as np

D_MODEL = 2048
BATCH = 2
SEQ = 8192
DEPTH = 1
DEC_BATCH = 8
DEC_SEQ = 64
PAST_LEN = 4096

CHUNK = 64
Q_BLOCK = 128
D_POOL = 1024
POOL_WINDOWS = (2, 4, 8, 16)
POOL_GROUP = D_POOL // len(POOL_WINDOWS)
POOL_STATE = max(POOL_WINDOWS) - 1
N_HEADS = 16
Q_LORA = 512
KV_LORA = 512
NOPE_DIM = 128
ROPE_DIM = 64
V_DIM = 128
QK_DIM = NOPE_DIM + ROPE_DIM
ATTN_SCALE = QK_DIM ** -0.5
ROPE_BASE = 10000.0
D_FF = 6144
CONV_W = 3
N_BRANCH = 2
D_IN = D_POOL + Q_LORA + KV_LORA + ROPE_DIM + N_BRANCH * D_MODEL
SPLITS = [D_POOL, D_POOL + Q_LORA, D_POOL + Q_LORA + KV_LORA, D_POOL + Q_LORA + KV_LORA + ROPE_DIM]
EPS = 1e-6

kernel_name = "hybrid_pool_mla_convffn_stream_step"


def rms_norm(x, g):
    xf = x.astype(jnp.float32)
    y = xf * lax.rsqrt(jnp.mean(xf * xf, axis=-1, keepdims=True) + EPS)
    return (y * g.astype(jnp.float32)).astype(x.dtype)


def rope_tables(pos0, length):
    pos = (pos0 + jnp.arange(length)).astype(jnp.float32)
    inv = ROPE_BASE ** (-(jnp.arange(ROPE_DIM // 2, dtype=jnp.float32) * 2.0 / ROPE_DIM))
    ang = pos[:, None] * inv[None, :]
    return jnp.cos(ang), jnp.sin(ang)


def apply_rope(x, cos, sin):
    xf = x.astype(jnp.float32)
    x1, x2 = xf[..., :ROPE_DIM // 2], xf[..., ROPE_DIM // 2:]
    return jnp.concatenate([x1 * cos - x2 * sin, x2 * cos + x1 * sin], axis=-1).astype(x.dtype)


def pool_mixer(z, prefix, pos0, pool_w, pool_scale):
    L = z.shape[1]
    ext = jnp.concatenate([prefix.astype(z.dtype), z], axis=1)
    zf = ext.astype(jnp.float32)
    csum = jnp.concatenate([jnp.zeros_like(zf[:, :1]), jnp.cumsum(zf, axis=1)], axis=1)
    end = csum[:, POOL_STATE + 1:]
    pos = pos0 + jnp.arange(L)
    outs = []
    for g, w in enumerate(POOL_WINDOWS):
        sl = slice(g * POOL_GROUP, (g + 1) * POOL_GROUP)
        start = csum[:, POOL_STATE + 1 - w: POOL_STATE + 1 - w + L, sl]
        cnt = jnp.minimum(pos + 1, w).astype(jnp.float32)[None, :, None]
        diff = (end[..., sl] - start) / cnt - zf[:, POOL_STATE:, sl]
        outs.append(jnp.einsum('blc,cd->bld', diff.astype(z.dtype), pool_w[g]))
    y = jnp.concatenate(outs, axis=-1) * pool_scale
    return y, ext[:, -POOL_STATE:]


def causal_dwconv(a, prefix, conv_w, conv_b):
    L = a.shape[1]
    ext = jnp.concatenate([prefix.astype(a.dtype), a], axis=1)
    y = conv_b + ext[:, 0:L] * conv_w[0]
    for k in range(1, CONV_W):
        y = y + ext[:, k:k + L] * conv_w[k]
    return y, ext[:, -(CONV_W - 1):]


def mla_block_causal(q_nope, q_rope, ckv, kr, w_uk, w_uv):
    B, L = ckv.shape[0], ckv.shape[1]
    k_nope = jnp.einsum('blc,chn->blhn', ckv, w_uk)
    v = jnp.einsum('blc,chv->blhv', ckv, w_uv)
    nb = L // Q_BLOCK
    key_chunk = jnp.arange(L) // CHUNK
    qn = q_nope.reshape(B, nb, Q_BLOCK, N_HEADS, NOPE_DIM).swapaxes(0, 1)
    qr = q_rope.reshape(B, nb, Q_BLOCK, N_HEADS, ROPE_DIM).swapaxes(0, 1)

    def one_block(args):
        qn_b, qr_b, i = args
        s = (jnp.einsum('bqhn,bkhn->bhqk', qn_b, k_nope, preferred_element_type=jnp.float32)
             + jnp.einsum('bqhr,bkr->bhqk', qr_b, kr, preferred_element_type=jnp.float32)) * ATTN_SCALE
        q_chunk = (i * Q_BLOCK + jnp.arange(Q_BLOCK)) // CHUNK
        mask = key_chunk[None, :] <= q_chunk[:, None]
        s = jnp.where(mask[None, None], s, -jnp.inf)
        p = jax.nn.softmax(s, axis=-1).astype(v.dtype)
        return jnp.einsum('bhqk,bkhv->bqhv', p, v)

    o = lax.map(one_block, (qn, qr, jnp.arange(nb)))
    return o.swapaxes(0, 1).reshape(B, L, N_HEADS, V_DIM)


def mla_with_cache(q_nope, q_rope, ckv_new, kr_new, ckv_cache, kr_cache, w_uk, w_uv):
    ckv = jnp.concatenate([ckv_cache.astype(ckv_new.dtype), ckv_new], axis=1)
    kr = jnp.concatenate([kr_cache.astype(kr_new.dtype), kr_new], axis=1)
    q_lat = jnp.einsum('bshn,chn->bshc', q_nope, w_uk)
    s = (jnp.einsum('bshc,btc->bhst', q_lat, ckv, preferred_element_type=jnp.float32)
         + jnp.einsum('bshr,btr->bhst', q_rope, kr, preferred_element_type=jnp.float32)) * ATTN_SCALE
    p = jax.nn.softmax(s, axis=-1).astype(ckv.dtype)
    o_lat = jnp.einsum('bhst,btc->bshc', p, ckv)
    return jnp.einsum('bshc,chv->bshv', o_lat, w_uv)


def trunk_layer(x, pool_prefix, conv_prefix, ckv_cache, kr_cache, pos0,
                norm1_g, w_in, pool_w, pool_scale, w_pool_out, q_norm_g, w_uq, kv_norm_g,
                w_uk, w_uv, w_mla_out, w_out, norm2_g, w_up, conv_w, conv_b, w_down):
    B, L, _ = x.shape
    u = rms_norm(x, norm1_g)
    proj = u @ w_in
    z_pool, c_q, c_kv, k_r, gate_logits = jnp.split(proj, SPLITS, axis=-1)
    y_pool, new_pool = pool_mixer(z_pool, pool_prefix, pos0, pool_w, pool_scale)
    br_a = y_pool @ w_pool_out
    c_q = rms_norm(c_q, q_norm_g)
    q = (c_q @ w_uq).reshape(B, L, N_HEADS, QK_DIM)
    cos, sin = rope_tables(pos0, L)
    q_nope = q[..., :NOPE_DIM]
    q_rope = apply_rope(q[..., NOPE_DIM:], cos[:, None, :], sin[:, None, :])
    c_kv = rms_norm(c_kv, kv_norm_g)
    k_r = apply_rope(k_r, cos, sin)
    if ckv_cache is None:
        o = mla_block_causal(q_nope, q_rope, c_kv, k_r, w_uk, w_uv)
    else:
        o = mla_with_cache(q_nope, q_rope, c_kv, k_r, ckv_cache, kr_cache, w_uk, w_uv)
    br_b = o.reshape(B, L, N_HEADS * V_DIM) @ w_mla_out
    gates = jax.nn.sigmoid(gate_logits.astype(jnp.float32)).astype(x.dtype).reshape(B, L, N_BRANCH, D_MODEL)
    h = x + (gates[:, :, 0] * br_a + gates[:, :, 1] * br_b) @ w_out
    up = rms_norm(h, norm2_g) @ w_up
    a, b = up[..., :D_FF], up[..., D_FF:]
    a_c, new_conv = causal_dwconv(a, conv_prefix, conv_w, conv_b)
    h = h + (jax.nn.gelu(a_c, approximate=False) * b) @ w_down
    return h, new_pool, c_kv, k_r, new_conv


def setup_inputs(seed: int = 0) -> dict:
    key = jax.random.key(seed)
    ks = jax.random.split(key, 32)
    f32 = jnp.float32
    nrm = lambda k, shape, scale: jax.random.normal(k, shape, f32) * scale
    gain = lambda k, shape: 1.0 + 0.02 * jax.random.normal(k, shape, f32)
    return {
        "x_prompt": nrm(ks[0], (BATCH, SEQ, D_MODEL), 1.0),
        "x_sample": nrm(ks[1], (DEC_BATCH, DEC_SEQ, D_MODEL), 1.0),
        "cache_ckv": nrm(ks[2], (DEPTH, DEC_BATCH, PAST_LEN, KV_LORA), 1.0),
        "cache_krope": nrm(ks[3], (DEPTH, DEC_BATCH, PAST_LEN, ROPE_DIM), 1.0),
        "state_pool": nrm(ks[4], (DEPTH, DEC_BATCH, POOL_STATE, D_POOL), 1.0),
        "state_conv": nrm(ks[5], (DEPTH, DEC_BATCH, CONV_W - 1, D_FF), 1.0),
        "norm1_g": gain(ks[6], (DEPTH, D_MODEL)),
        "w_in": nrm(ks[7], (DEPTH, D_MODEL, D_IN), D_MODEL ** -0.5),
        "pool_w": nrm(ks[8], (DEPTH, len(POOL_WINDOWS), POOL_GROUP, POOL_GROUP), POOL_GROUP ** -0.5),
        "pool_scale": gain(ks[9], (DEPTH, D_POOL)),
        "w_pool_out": nrm(ks[10], (DEPTH, D_POOL, D_MODEL), D_POOL ** -0.5),
        "q_norm_g": gain(ks[11], (DEPTH, Q_LORA)),
        "w_uq": nrm(ks[12], (DEPTH, Q_LORA, N_HEADS * QK_DIM), Q_LORA ** -0.5),
        "kv_norm_g": gain(ks[13], (DEPTH, KV_LORA)),
        "w_uk": nrm(ks[14], (DEPTH, KV_LORA, N_HEADS, NOPE_DIM), KV_LORA ** -0.5),
        "w_uv": nrm(ks[15], (DEPTH, KV_LORA, N_HEADS, V_DIM), KV_LORA ** -0.5),
        "w_mla_out": nrm(ks[16], (DEPTH, N_HEADS * V_DIM, D_MODEL), (N_HEADS * V_DIM) ** -0.5),
        "w_out": nrm(ks[17], (DEPTH, D_MODEL, D_MODEL), D_MODEL ** -0.5),
        "norm2_g": gain(ks[18], (DEPTH, D_MODEL)),
        "w_up": nrm(ks[19], (DEPTH, D_MODEL, 2 * D_FF), D_MODEL ** -0.5),
        "conv_w": nrm(ks[20], (DEPTH, CONV_W, D_FF), CONV_W ** -0.5),
        "conv_b": nrm(ks[21], (DEPTH, D_FF), 0.01),
        "w_down": nrm(ks[22], (DEPTH, D_FF, D_MODEL), D_FF ** -0.5),
        "final_g": gain(ks[23], (D_MODEL,)),
    }


def reference(x_prompt, x_sample, cache_ckv, cache_krope, state_pool, state_conv,
              norm1_g, w_in, pool_w, pool_scale, w_pool_out, q_norm_g, w_uq, kv_norm_g,
              w_uk, w_uv, w_mla_out, w_out, norm2_g, w_up, conv_w, conv_b, w_down, final_g):
    past_len = cache_ckv.shape[2]
    bp = x_prompt.shape[0]
    hp, hs = x_prompt, x_sample
    p_ckv, p_kr, p_pool, p_conv = [], [], [], []
    s_ckv, s_kr, s_pool, s_conv = [], [], [], []
    for l in range(DEPTH):
        lw = (norm1_g[l], w_in[l], pool_w[l], pool_scale[l], w_pool_out[l], q_norm_g[l], w_uq[l],
              kv_norm_g[l], w_uk[l], w_uv[l], w_mla_out[l], w_out[l], norm2_g[l], w_up[l],
              conv_w[l], conv_b[l], w_down[l])
        pool0 = jnp.zeros((bp, POOL_STATE, D_POOL), hp.dtype)
        conv0 = jnp.zeros((bp, CONV_W - 1, D_FF), hp.dtype)
        hp, pp, pc, pk, pcv = trunk_layer(hp, pool0, conv0, None, None, 0, *lw)
        hs, sp, sc, sk, scv = trunk_layer(hs, state_pool[l], state_conv[l], cache_ckv[l], cache_krope[l],
                                          past_len, *lw)
        p_ckv.append(pc); p_kr.append(pk); p_pool.append(pp); p_conv.append(pcv)
        s_ckv.append(sc); s_kr.append(sk); s_pool.append(sp); s_conv.append(scv)
    y_prompt = rms_norm(hp, final_g)
    y_sample = rms_norm(hs, final_g)
    return (y_prompt, y_sample,
            jnp.stack(p_ckv), jnp.stack(p_kr), jnp.stack(p_pool), jnp.stack(p_conv),
            jnp.stack(s_ckv), jnp.stack(s_kr), jnp.stack(s_pool), jnp.stack(s_conv))
```

```python
import numpy as np
from contextlib import ExitStack
import concourse.bass as bass
import concourse.mybir as mybir
from concourse.bass_utils import run_bass_kernel_spmd

F32 = mybir.dt.float32
BF16 = mybir.dt.bfloat16
AF = mybir.ActivationFunctionType
ALU = mybir.AluOpType

D = 2048
NH = 16
DFF = 6144
EPS = 1e-6
SCALE = 192 ** -0.5
NEG = -30000.0
WIN = 8192
NKT_W = 64
NKT_S = 33
NKT = NKT_W + NKT_S
NK = NKT * 128
PIECE = 4096
ARENA = 160 * 1024


class Sem:
    def __init__(self, h):
        self.h = h
        self.count = 0


class Obj:
    def __init__(self, name):
        self.name = name
        self.last_w = None
        self.reads = {}
        self.dsem = None
        self.excl = False


class Eng:
    def __init__(self, name, e):
        self.name = name
        self.e = e
        self.sem = None
        self.n = 0
        self.seen = {}
        self.pend_r = []
        self.pend_w = []


class Tracker:
    def __init__(self, nc, es):
        self.nc = nc
        self.es = es
        self.nsem = 0
        self.PE = Eng("pe", nc.tensor)
        self.ACT = Eng("act", nc.scalar)
        self.DVE = Eng("dve", nc.vector)
        self.POOL = Eng("pool", nc.gpsimd)
        self.SP = Eng("sp", nc.sync)
        self.engs = [self.PE, self.ACT, self.DVE, self.POOL, self.SP]
        self.objs = []
        self.sem_owner = {}
        self.byname = {}
        self.dsems = []

    def new_sem(self):
        self.nsem += 1
        h = self.es.enter_context(self.nc.semaphore("s%d" % self.nsem))
        return Sem(h)

    def obj(self, name):
        if name in self.byname:
            return self.byname[name]
        o = Obj(name)
        self.byname[name] = o
        self.objs.append(o)
        return o

    def _wait(self, eng, deps):
        for s, v in deps.values():
            if eng.seen.get(id(s), 0) < v:
                eng.e.wait_ge(s.h, v)
                eng.seen[id(s)] = v

    def _deps(self, eng, reads, writes):
        deps = {}

        def add(ev):
            s, v = ev
            if eng is self.PE and s is self.PE.sem:
                return
            k = id(s)
            if k not in deps or deps[k][1] < v:
                deps[k] = (s, v)
        for o in reads:
            if o.last_w is not None:
                add(o.last_w)
            if o.excl:
                for k, ev in o.reads.items():
                    if self.sem_owner.get(k) is not eng:
                        add(ev)
        for o in writes:
            if o.last_w is not None:
                add(o.last_w)
            for ev in o.reads.values():
                add(ev)
        return deps

    def op(self, eng, fn, reads=(), writes=(), signal=True):
        self._wait(eng, self._deps(eng, reads, writes))
        inst = fn()
        eng.pend_r.extend(reads)
        eng.pend_w.extend(writes)
        if signal:
            if eng.sem is None or eng.n >= 30000:
                eng.sem = self.new_sem()
                self.sem_owner[id(eng.sem)] = eng
                eng.n = 0
            eng.n += 1
            inst.then_inc(eng.sem.h, 1)
            ev = (eng.sem, eng.n)
            for o in eng.pend_r:
                o.reads[id(eng.sem)] = ev
            for o in eng.pend_w:
                o.last_w = ev
                o.reads = {}
            eng.pend_r = []
            eng.pend_w = []
        return inst

    def pe(self, fn, r=(), w=(), signal=True):
        return self.op(self.PE, fn, r, w, signal)

    def act(self, fn, r=(), w=()):
        return self.op(self.ACT, fn, r, w)

    def dve(self, fn, r=(), w=()):
        return self.op(self.DVE, fn, r, w)

    def pool(self, fn, r=(), w=()):
        return self.op(self.POOL, fn, r, w)

    def dma(self, q, out, in_, reads, writes, sobj, **kw):
        self._wait(q, self._deps(q, reads, writes))
        if sobj.dsem is None:
            sobj.dsem = self.new_sem()
            self.dsems.append(sobj.dsem)
        s = sobj.dsem
        s.count += 16
        q.e.dma_start(out=out, in_=in_, **kw).then_inc(s.h, 16)
        ev = (s, s.count)
        for o in reads:
            o.reads[id(s)] = ev
        for o in writes:
            o.last_w = ev
            o.reads = {}

    def load(self, out, in_, wobj, robjs=(), q=None, war=(), **kw):
        self.dma(q or self.SP, out, in_, list(robjs), [wobj] + list(war), wobj, **kw)

    def store(self, out, in_, robj, wobjs=(), q=None, **kw):
        self.dma(q or self.SP, out, in_, [robj], list(wobjs), robj, **kw)

    def soft_barrier(self, dma_objs=()):
        evs = {}
        for o in dma_objs:
            if o.dsem is not None and o.dsem.count:
                evs[id(o.dsem)] = (o.dsem, o.dsem.count)
        comp = [self.PE, self.ACT, self.DVE, self.POOL]
        for e in comp:
            assert not e.pend_r and not e.pend_w, e.name
            if e.sem is not None and e.n > 0:
                evs[id(e.sem)] = (e.sem, e.n)
        for e in comp:
            d = {k: v for k, v in evs.items() if not (e is self.PE and e.sem is not None and k == id(e.sem))}
            self._wait(e, d)

    def barrier(self):
        evs = {}
        for e in self.engs:
            assert not e.pend_r and not e.pend_w, e.name
            if e.sem is not None and e.n > 0:
                evs[id(e.sem)] = (e.sem, e.n)
        for s in self.dsems:
            if s.count:
                evs[id(s)] = (s, s.count)
        for e in self.engs:
            d = {k: v for k, v in evs.items() if not (e.sem is not None and k == id(e.sem) and e is self.PE)}
            self._wait(e, d)
        for o in self.objs:
            o.last_w = None
            o.reads = {}


class Ring:
    def __init__(self, T, nc, es, name, n, shape, dt):
        self.T = T
        self.slots = []
        for i in range(n):
            t = es.enter_context(nc.sbuf_tensor("%s%d" % (name, i), shape, dt))
            self.slots.append((t, T.obj("%s%d" % (name, i))))
        self.i = 0

    def next(self):
        s = self.slots[self.i % len(self.slots)]
        self.i += 1
        return s


def build_nc(stage=99, dbg=False):
    nc = bass.Bass("TRN2", target_bir_lowering=False)

    def din(name, shape, dt=F32):
        return nc.dram_tensor(name, list(shape), dt, kind="ExternalInput").ap()

    def dout(name, shape, dt=F32):
        return nc.dram_tensor(name, list(shape), dt, kind="ExternalOutput").ap()

    def dint(name, shape, dt=BF16):
        return nc.dram_tensor(name, list(shape), dt, kind="ExternalOutput" if dbg else "Internal").ap()

    x_win = din("x_win", [WIN, D])
    xs_in = din("xs", [128, D])
    c_ckv = din("c_ckv", [4096, 512])
    c_kr = din("c_kr", [4096, 64])
    st_pool = din("st_pool", [16, 1024])
    st_conv = din("st_conv", [2, DFF])
    ropek = din("ropek", [65 * 128, 64])
    ropeq = din("ropeq", [5, 128, 512])
    corr = din("corr", [128, 4 * 16])
    nullrow = din("nullrow", [1, NK])
    w_kvr = din("w_kvr", [D, 576])
    w_pq = din("w_pq", [D, 1536])
    w_mg = din("w_mg", [16 * 7168, 128])
    w_uqp = din("w_uqp", [512, 4096])
    w_uk = din("w_uk", [512, 2048])
    w_uv = din("w_uv", [512, 2048])
    w_pool = din("w_pool", [1024, 256])
    w_out = din("w_out", [D, D])
    w_upp = din("w_upp", [D, 2 * DFF])
    w_down = din("w_down", [DFF, D])
    g1T = din("g1T", [128, 16])
    g2T = din("g2T", [128, 16])
    qgT = din("qgT", [128, 4])
    pscT = din("pscT", [128, 8])
    kvg_rep = din("kvg_rep", [128, 512])
    g1_rep = din("g1_rep", [128, D])
    fg_rep = din("fg_rep", [128, D])
    cwT = din("cwT", [128, 48 * 3])
    cbT = din("cbT", [128, 48])
    ident_in = din("ident", [128, 128])
    sel_in = din("sel", [128, 128])
    y_p = dout("y_p", [2048, D])
    y_s = dout("y_s", [64, D])
    o_pckv = dout("o_pckv", [2048, 512])
    o_pkr = dout("o_pkr", [2048, 64])
    o_ppool = dout("o_ppool", [15, 1024])
    o_pconv = dout("o_pconv", [2, DFF])
    o_sckv = dout("o_sckv", [64, 512])
    o_skr = dout("o_skr", [64, 64])
    o_spool = dout("o_spool", [15, 1024])
    o_sconv = dout("o_sconv", [2, DFF])
    b_kvr = dint("b_kvr", [D, 576])
    b_pq = dint("b_pq", [D, 1536])
    b_mg = dint("b_mg", [16 * 7168, 128])
    b_uqp = dint("b_uqp", [512, 4096])
    b_uk = dint("b_uk", [512, 2048])
    b_uv = dint("b_uv", [512, 2048])
    b_pool = dint("b_pool", [1024, 256])
    b_out = dint("b_out", [D, D])
    b_upp = dint("b_upp", [D, 2 * DFF])
    b_down = dint("b_down", [DFF, D])
    Ksc = dint("Ksc", [NH, 128, NK])
    Vsc = dint("Vsc", [NH, 128, NKT, 128])
    Krsc = dint("Krsc", [65, NK])

    with ExitStack() as es:
        T = Tracker(nc, es)
        E = es.enter_context
        PE, ACT, DVE, POOL, SP = T.PE, T.ACT, T.DVE, T.POOL, T.SP

        def sb(name, shape, dt):
            return E(nc.sbuf_tensor(name, list(shape), dt))

        dbg_seen = set()

        def dbgout(name, ap, obj, shape, dt):
            if not dbg or name in dbg_seen:
                return
            dbg_seen.add(name)
            dd = nc.dram_tensor("dbg_" + name, list(shape), dt, kind="ExternalOutput").ap()
            T.store(dd, ap, obj)

        ident_f = sb("ident_f", [128, 128], F32)
        ident_b = sb("ident_b", [128, 128], BF16)
        ones_f = sb("ones_f", [128, 128], F32)
        sel_f = sb("sel_f", [128, 128], F32)
        ones_b = sb("ones_b", [128, 128], BF16)
        g1s = sb("g1s", [128, 16], F32)
        g2s = sb("g2s", [128, 16], F32)
        qgs = sb("qgs", [128, 4], F32)
        pscs = sb("pscs", [128, 8], F32)
        cws = sb("cws", [128, 48 * 3], F32)
        cbs = sb("cbs", [128, 48], F32)
        corrs = sb("corrs", [128, 64], F32)
        aprev_p = sb("aprev_p", [128, 48, 2], F32)
        aprev_s = sb("aprev_s", [128, 48, 2], F32)
        zprev_p = sb("zprev_p", [128, 8, 16], F32)
        zprev_s = sb("zprev_s", [128, 8, 16], F32)
        stat = sb("stat", [128, 64], F32)
        consts = T.obj("consts")
        o_stat = T.obj("stat")
        o_aprev_p = T.obj("aprev_p")
        o_aprev_s = T.obj("aprev_s")
        o_zprev_p = T.obj("zprev_p")
        o_zprev_s = T.obj("zprev_s")
        wring = Ring(T, nc, es, "wr", 3, [128, PIECE], BF16)
        arena = sb("arena", [128, ARENA], mybir.dt.uint8)
        ps = [E(nc.psum_tensor("ps%d" % i, [128, 512], F32)) for i in range(8)]
        o_ps = [T.obj("ps%d" % i) for i in range(8)]
        for o in o_ps:
            o.excl = True

        class Carver:
            def __init__(self):
                self.off = 0

            def reset(self):
                self.off = 0

            def get(self, name, shape, dt):
                esz = 4 if dt == F32 else 2
                n = int(np.prod(shape[1:]))
                nbytes = (n * esz + 31) // 32 * 32
                assert self.off + nbytes <= ARENA, (name, self.off, nbytes)
                v = arena[0:shape[0], self.off:self.off + n * esz].bitcast(dt)
                self.off += nbytes
                if len(shape) == 3:
                    v = v.rearrange("p (a b) -> p a b", a=shape[1])
                elif len(shape) == 4:
                    v = v.rearrange("p (a b c) -> p a b c", a=shape[1], b=shape[2])
                return v, T.obj(name)

        CV = Carver()

        for dst, src in [(ident_f, ident_in), (sel_f, sel_in), (g1s, g1T), (g2s, g2T), (qgs, qgT), (pscs, pscT),
                         (cws, cwT), (cbs, cbT), (corrs, corr)]:
            T.load(dst[:], src, consts)
        T.dve(lambda: nc.vector.tensor_copy(out=ident_b[:], in_=ident_f[:]), [consts], [consts])
        T.dve(lambda: nc.vector.memset(ones_f[:], 1.0), [], [consts])
        T.dve(lambda: nc.vector.memset(ones_b[:], 1.0), [], [consts])

        wobj = {}

        import os
        CONV = os.environ.get("DBG_CONV", "all")

        def convert(name, src, dst, rows, cols, r_lo=0, r_hi=None):
            o = T.obj("cv_" + name)
            wobj[name] = o
            if CONV != "all" and name not in CONV.split(","):
                return
            rb = 256 if cols > 2048 else 512
            for r0 in range(r_lo, rows if r_hi is None else r_hi, rb):
                T.dma(POOL, dst[r0:r0 + rb, :], src[r0:r0 + rb, :], [], [o], o, max_dma_last_dim=2048 * 4)
                if o.dsem.count > 3 * 16:
                    POOL.e.wait_ge(o.dsem.h, o.dsem.count - 3 * 16)
        convert("kvr", w_kvr, b_kvr, D, 576)
        convert("uk", w_uk, b_uk, 512, 2048)
        convert("uv", w_uv, b_uv, 512, 2048)
        convert("pq", w_pq, b_pq, D, 1536)
        convert("pool", w_pool, b_pool, 1024, 256)
        convert("uqp", w_uqp, b_uqp, 512, 4096)
        convert("mg", w_mg, b_mg, 16 * 7168, 128)
        convert("out", w_out, b_out, D, D)

        def convert_late():
            for q in range(4):
                convert("upp%d" % q, w_upp[:, q * 3072:(q + 1) * 3072], b_upp[:, q * 3072:(q + 1) * 3072], D, 3072)
            for k6 in range(6):
                convert("down%d" % k6, w_down, b_down, DFF, D, r_lo=k6 * 1024, r_hi=(k6 + 1) * 1024)
        convert_late()

        def wload(name, src_ap, shape):
            t, o = wring.next()
            n = int(np.prod(shape[1:]))
            assert n <= PIECE
            v = t[:, 0:n]
            if len(shape) == 3:
                v = v.rearrange("p (a b) -> p a b", a=shape[1])
            T.load(v, src_ap, o, [wobj[name]])
            return v, o

        if stage <= 0:
            T.barrier()
            return nc
        def rstd_from_ss(ss_ap, n, dim, r_objs, w_obj, out_ap):
            T.act(lambda: nc.scalar.activation(out=out_ap, in_=ss_ap, func=AF.Sqrt, bias=eps_t[0:n, 0:1],
                                               scale=1.0 / dim), r_objs + [consts], [w_obj])
            T.dve(lambda: nc.vector.reciprocal(out=out_ap, in_=out_ap), [w_obj], [w_obj])

        eps_t = sb("eps_t", [128, 1], F32)
        T.dve(lambda: nc.vector.memset(eps_t[:], EPS), [], [consts])

        CV.reset()
        wk_kvr, o_wkvr = CV.get("wk_kvr", [128, 16, 576], BF16)
        wk_uk, o_wuk = CV.get("wk_uk", [128, 4, 2048], BF16)
        wk_uv, o_wuv = CV.get("wk_uv", [128, 4, 2048], BF16)
        kvgs, o_kvgs = CV.get("kvgs", [128, 512], F32)
        xk = [CV.get("xk%d" % i, [128, D], F32) for i in range(4)]
        ukbs = [CV.get("ukb%d" % i, [128, D], BF16) for i in range(2)]
        uTks = [CV.get("uTk%d" % i, [128, 16, 128], BF16) for i in range(2)]
        stks = [CV.get("stk%d" % i, [128, 4], F32) for i in range(2)]
        ckf, o_ckf = CV.get("ckf", [128, 512], F32)
        ckb, o_ckb = CV.get("ckb", [128, 512], BF16)
        krf, o_krf = CV.get("krf", [128, 64], F32)
        krb, o_krb = CV.get("krb", [128, 128], BF16)
        rks = [CV.get("rk%d" % i, [128, 64], F32) for i in range(4)]
        rtmp, o_rtmp = CV.get("rtmp", [128, 4, 32], F32)
        ckvT, o_ckvT = CV.get("ckvT", [128, 4, 512], BF16)
        krTs, o_krTs = CV.get("krTs", [65, 512], BF16)
        nullf, o_nullf = CV.get("nullf", [65, 512], F32)
        kst, o_kst = CV.get("kst", [128, 16, 512], BF16)
        vst, o_vst = CV.get("vst", [128, 16, 4, 128], BF16)
        ccf, o_ccf = CV.get("ccf", [128, 4, 512], F32)
        ccb, o_ccb = CV.get("ccb", [128, 4, 512], BF16)
        ckrf, o_ckrf = CV.get("ckrf", [128, 4, 64], F32)
        ckrb, o_ckrb = CV.get("ckrb", [128, 4, 128], BF16)

        T.load(wk_kvr, b_kvr.rearrange("(c p) n -> p c n", p=128), o_wkvr, [wobj["kvr"]])
        T.load(wk_uk, b_uk.rearrange("(c p) n -> p c n", p=128), o_wuk, [wobj["uk"]])
        T.load(wk_uv, b_uv.rearrange("(c p) n -> p c n", p=128), o_wuv, [wobj["uv"]])
        T.load(kvgs, kvg_rep, o_kvgs)
        for c in range(16):
            T.dve(lambda c=c: nc.vector.tensor_scalar(out=wk_kvr[:, c, :], in0=wk_kvr[:, c, :], scalar1=g1s[:, c:c + 1],
                                                      scalar2=None, op0=ALU.mult), [o_wkvr, consts], [o_wkvr])
        T.dve(lambda: nc.vector.memset(krb, 0.0), [], [o_krb])
        T.dve(lambda: nc.vector.memset(ckrb, 0.0), [], [o_ckrb])

        def kv_load(x_src, rope_src, ls):
            T.load(xk[ls][0], x_src, xk[ls][1])
            T.load(rks[ls][0], rope_src, rks[ls][1])

        def kv_front_a(x_src, rope_src, slot, ls=None):
            ukb, o_ukb = ukbs[slot]
            stk, o_stk = stks[slot]
            if ls is None:
                ls = slot
                kv_load(x_src, rope_src, ls)
            xt, o_x = xk[ls]
            T.act(lambda: nc.scalar.activation(out=ukb, in_=xt, func=AF.Square, accum_out=stk[:, 0:1]), [o_x], [o_ukb, o_stk])
            rstd_from_ss(stk[:, 0:1], 128, D, [o_stk], o_stk, stk[:, 1:2])
            T.act(lambda: nc.scalar.activation(out=ukb, in_=xt, func=AF.Copy, scale=stk[:, 1:2]), [o_x, o_stk], [o_ukb])

        def kv_front_b(slot):
            ukb, o_ukb = ukbs[slot]
            uTk, o_uTk = uTks[slot]
            for half in range(2):
                pb = ps[6 + half][:].bitcast(BF16)
                for j in range(8):
                    c = half * 8 + j
                    T.pe(lambda c=c, j=j, pb=pb: nc.tensor.transpose(out=pb[:, j * 128:(j + 1) * 128],
                                                                      in_=ukb[:, c * 128:(c + 1) * 128],
                                                                      identity=ident_b[:]),
                         [o_ukb, consts], [o_ps[6 + half]], signal=(j == 7))
                src = pb.rearrange("p (a b) -> p a b", a=8)
                if half == 0:
                    T.dve(lambda: nc.vector.tensor_copy(out=uTk[:, 0:8, :], in_=src), [o_ps[6]], [o_uTk])
                else:
                    T.act(lambda: nc.scalar.copy(out=uTk[:, 8:16, :], in_=src), [o_ps[7]], [o_uTk])

        def kv_front(x_src, rope_src, slot):
            kv_front_a(x_src, rope_src, slot)
            kv_front_b(slot)

        def kv_back_a(slot):
            uTk, o_uTk = uTks[slot]
            for c in range(16):
                T.pe(lambda c=c: nc.tensor.matmul(ps[0][:], lhsT=uTk[:, c, :], rhs=wk_kvr[:, c, 0:512],
                                                  start=(c == 0), stop=(c == 15)),
                     [o_uTk, o_wkvr], [o_ps[0]], signal=False)
                T.pe(lambda c=c: nc.tensor.matmul(ps[1][:, 0:64], lhsT=uTk[:, c, :], rhs=wk_kvr[:, c, 512:576],
                                                  start=(c == 0), stop=(c == 15)),
                     [o_uTk, o_wkvr], [o_ps[1]], signal=(c == 15))

        def kv_back(slot, ti, ckv_out, kr_out, nrows):
            kv_back_a(slot)
            kv_back_b(slot, ti, ckv_out, kr_out, nrows)

        def kv_back_b(slot, ti, ckv_out, kr_out, nrows, ls=None):
            stk, o_stk = stks[slot]
            rk, o_rk = rks[slot if ls is None else ls]
            T.act(lambda: nc.scalar.activation(out=ckf, in_=ps[0][:], func=AF.Square, accum_out=stk[:, 2:3]),
                  [o_ps[0]], [o_ckf, o_stk])
            rstd_from_ss(stk[:, 2:3], 128, 512, [o_stk], o_stk, stk[:, 3:4])
            T.dve(lambda: nc.vector.scalar_tensor_tensor(out=ckf, in0=ps[0][:], scalar=stk[:, 3:4], in1=kvgs,
                                                         op0=ALU.mult, op1=ALU.mult),
                  [o_ps[0], o_stk, o_kvgs], [o_ckf])
            T.act(lambda: nc.scalar.copy(out=ckb, in_=ckf), [o_ckf], [o_ckb])
            if ckv_out is not None:
                T.store(ckv_out, ckf[0:nrows, :], o_ckf)
            x1 = ps[1][:, 0:32]
            x2 = ps[1][:, 32:64]
            cs = rk[:, 0:32]
            sn = rk[:, 32:64]
            T.dve(lambda: nc.vector.tensor_tensor(out=rtmp[:, 0, :], in0=x1, in1=cs, op=ALU.mult), [o_ps[1], o_rk], [o_rtmp])
            T.dve(lambda: nc.vector.tensor_tensor(out=rtmp[:, 1, :], in0=x2, in1=sn, op=ALU.mult), [o_ps[1], o_rk], [o_rtmp])
            T.dve(lambda: nc.vector.tensor_tensor(out=rtmp[:, 2, :], in0=x2, in1=cs, op=ALU.mult), [o_ps[1], o_rk], [o_rtmp])
            T.dve(lambda: nc.vector.tensor_tensor(out=rtmp[:, 3, :], in0=x1, in1=sn, op=ALU.mult), [o_ps[1], o_rk], [o_rtmp])
            T.dve(lambda: nc.vector.tensor_tensor(out=krf[:, 0:32], in0=rtmp[:, 0, :], in1=rtmp[:, 1, :], op=ALU.subtract),
                  [o_rtmp], [o_krf])
            T.dve(lambda: nc.vector.tensor_tensor(out=krf[:, 32:64], in0=rtmp[:, 2, :], in1=rtmp[:, 3, :], op=ALU.add),
                  [o_rtmp], [o_krf])
            T.dve(lambda: nc.vector.tensor_copy(out=krb[:, 0:64], in_=krf), [o_krf], [o_krb])
            if kr_out is not None:
                T.store(kr_out, krf[0:nrows, :], o_krf)
            kv_transposes(ckb, o_ckb, krb, o_krb, ti)

        def kv_transposes(cb_ap, o_cb, kb_ap, o_kb, ti):
            pb = ps[5][:].bitcast(BF16)
            for c in range(4):
                T.pe(lambda c=c: nc.tensor.transpose(out=pb[:, c * 128:(c + 1) * 128], in_=cb_ap[:, c * 128:(c + 1) * 128],
                                                      identity=ident_b[:]), [o_cb, consts], [o_ps[5]], signal=False)
            T.pe(lambda: nc.tensor.transpose(out=pb[:, 512:640], in_=kb_ap, identity=ident_b[:]),
                 [o_kb, consts], [o_ps[5]])
            T.dve(lambda: nc.vector.tensor_copy(out=ckvT[:, :, ti * 128:(ti + 1) * 128],
                                                in_=pb[:, 0:512].rearrange("p (a b) -> p a b", a=4)),
                  [o_ps[5]], [o_ckvT])
            T.dve(lambda: nc.vector.tensor_copy(out=krTs[0:64, ti * 128:(ti + 1) * 128], in_=pb[0:64, 512:640]),
                  [o_ps[5]], [o_krTs])

        def kv_group_finish(gt0, ntiles):
            nk = ntiles * 128
            k0 = gt0 * 128
            T.load(nullf[64:65, 0:nk], nullrow[0:1, k0:k0 + nk], o_nullf)
            T.act(lambda: nc.scalar.copy(out=krTs[64:65, 0:nk], in_=nullf[64:65, 0:nk]), [o_nullf], [o_krTs])
            T.store(Krsc[:, k0:k0 + nk], krTs[:, 0:nk], o_krTs)
            bank = 0
            for h in range(NH):
                b = 2 + (h % 3)
                for c in range(4):
                    T.pe(lambda c=c, h=h, b=b: nc.tensor.matmul(ps[b][:, 0:nk], lhsT=wk_uk[:, c, h * 128:(h + 1) * 128],
                                                                rhs=ckvT[:, c, 0:nk], start=(c == 0), stop=(c == 3)),
                         [o_wuk, o_ckvT], [o_ps[b]], signal=(c == 3))
                if h % 2 == 0:
                    T.act(lambda h=h, b=b: nc.scalar.copy(out=kst[:, h, 0:nk], in_=ps[b][:, 0:nk]), [o_ps[b]], [o_kst])
                else:
                    T.dve(lambda h=h, b=b: nc.vector.tensor_copy(out=kst[:, h, 0:nk], in_=ps[b][:, 0:nk]), [o_ps[b]], [o_kst])
            T.store(Ksc[:, :, k0:k0 + nk].rearrange("h p k -> p h k"), kst[:, :, 0:nk], o_kst)
            i = 0
            for ti in range(ntiles):
                for hb in range(4):
                    b = 2 + (i % 3)
                    i += 1
                    for c in range(4):
                        T.pe(lambda c=c, hb=hb, b=b, ti=ti: nc.tensor.matmul(
                            ps[b][:], lhsT=ckvT[:, c, ti * 128:(ti + 1) * 128], rhs=wk_uv[:, c, hb * 512:(hb + 1) * 512],
                            start=(c == 0), stop=(c == 3)), [o_wuv, o_ckvT], [o_ps[b]], signal=(c == 3))
                    src = ps[b][:].rearrange("p (a b) -> p a b", a=4)
                    if i % 2 == 0:
                        T.act(lambda hb=hb, ti=ti, src=src: nc.scalar.copy(out=vst[:, hb * 4:(hb + 1) * 4, ti, :], in_=src),
                              [o_ps[b]], [o_vst])
                    else:
                        T.dve(lambda hb=hb, ti=ti, src=src: nc.vector.tensor_copy(out=vst[:, hb * 4:(hb + 1) * 4, ti, :], in_=src),
                              [o_ps[b]], [o_vst])
            T.store(Vsc.rearrange("h p t v -> p h (t v)")[:, :, gt0 * 128:(gt0 + ntiles) * 128],
                    vst.rearrange("p h t v -> p h (t v)")[:, :, 0:ntiles * 128], o_vst)

        ngr = 16 if stage > 1 else 1
        wt = list(range(ngr * 4))
        def kvl(t):
            if t < len(wt):
                kv_load(x_win[t * 128:(t + 1) * 128, :], ropek[t * 128:(t + 1) * 128, :], t % 4)
        for t0_ in range(3):
            kvl(t0_)
        kv_front_a(None, None, 0, ls=0)
        kv_front_b(0)
        for t in wt:
            nxt = t + 1 < len(wt)
            kvl(t + 3)
            if nxt:
                kv_front_a(None, None, (t + 1) % 2, ls=(t + 1) % 4)
            kv_back_a(t % 2)
            if nxt:
                kv_front_b((t + 1) % 2)
            own = t >= 48
            kv_back_b(t % 2, t % 4, o_pckv[(t - 48) * 128:(t - 47) * 128, :] if own else None,
                      o_pkr[(t - 48) * 128:(t - 47) * 128, :] if own else None, 128, ls=t % 4)
            if t % 4 == 3:
                kv_group_finish((t // 4) * 4, 4)
        if stage <= 1:
            T.barrier()
            return nc
        for gi in range(8):
            r0 = gi * 512
            T.load(ccf, c_ckv[r0:r0 + 512, :].rearrange("(t p) n -> p t n", p=128), o_ccf)
            T.load(ckrf, c_kr[r0:r0 + 512, :].rearrange("(t p) n -> p t n", p=128), o_ckrf)
            T.act(lambda: nc.scalar.copy(out=ccb, in_=ccf), [o_ccf], [o_ccb])
            T.dve(lambda: nc.vector.tensor_copy(out=ckrb[:, :, 0:64], in_=ckrf), [o_ckrf], [o_ckrb])
            for ti in range(4):
                kv_transposes(ccb[:, ti, :], o_ccb, ckrb[:, ti, :], o_ckrb, ti)
            kv_group_finish(NKT_W + gi * 4, 4)
        kv_front(xs_in, ropek[64 * 128:65 * 128, :], 0)
        kv_back(0, 0, o_sckv, o_skr, 64)
        kv_group_finish(NKT_W + 32, 1)

        T.barrier()

        def run_block(bi):
            N = 256 if bi == 0 else 512
            NT = N // 128
            CV.reset()
            xh, o_xh = CV.get("xh", [128, 4, D], F32)
            qn_v = arena[:, 0:16 * 512 * 2].bitcast(BF16).rearrange("p (a b) -> p a b", a=16)
            qr_v = arena[0:65, 16 * 512 * 2:32 * 512 * 2].bitcast(BF16).rearrange("p (a b) -> p a b", a=16)
            o_qn = T.obj("qn")
            o_qr = T.obj("qr")
            big, o_big = CV.get("big", [128, 26 * 1024], BF16)
            a2 = CV.off - 52 * 1024

            def a2view(off, shape, dt, parts=128):
                esz = 4 if dt == F32 else 2
                n = int(np.prod(shape[1:]))
                v = arena[0:parts, a2 + off:a2 + off + n * esz].bitcast(dt)
                if len(shape) == 3:
                    v = v.rearrange("p (a b) -> p a b", a=shape[1])
                return v
            uT, o_uT = CV.get("uT", [128, 16, 512], BF16)
            yT, o_yT = CV.get("yT", [128, 8, 512], BF16)
            ub, o_ub = CV.get("ub", [128, D], BF16)
            sq, o_sq = CV.get("sq", [128, 512], F32)
            rsb, o_rsb = CV.get("rsb", [128, 512], F32)
            cqn, o_cqn = CV.get("cqn", [128, 4, 512], BF16)
            rq, o_rq = CV.get("rq", [128, 512], F32)
            t1, o_t1 = CV.get("t1", [128, 512], F32)
            t2, o_t2 = CV.get("t2", [128, 512], F32)
            t3, o_t3 = CV.get("t3", [128, 514], F32)
            sga, o_sga = CV.get("sga", [128, 512], BF16)
            sgb, o_sgb = CV.get("sgb", [128, 512], BF16)
            PT = [CV.get("PT%d" % i, [128, 512], BF16) for i in range(3)]
            rden, o_rden = CV.get("rden", [128, 512], F32)

            if bi == 0:
                T.load(xh[:, 0, :], x_win[47 * 128:48 * 128, :], o_xh, war=[o_qn, o_qr])
                T.load(xh[:, 1, :], xs_in, o_xh, war=[o_qn, o_qr])
            else:
                r0 = (48 + (bi - 1) * 4) * 128
                T.load(xh[:, 0:4, :], x_win[r0:r0 + 512, :].rearrange("(t p) n -> p t n", p=128), o_xh,
                       war=[o_qn, o_qr])
            T.load(rq, ropeq[bi], o_rq)

            def norm_transposes(gs):
                for tt in range(NT):
                    ss = stat[:, 8 + 2 * tt:9 + 2 * tt]
                    rs = stat[:, 9 + 2 * tt:10 + 2 * tt]
                    T.act(lambda: nc.scalar.activation(out=ub, in_=xh[:, tt, :], func=AF.Square, accum_out=ss),
                          [o_xh], [o_ub, o_stat])
                    rstd_from_ss(ss, 128, D, [o_stat], o_stat, rs)
                    T.act(lambda: nc.scalar.activation(out=ub, in_=xh[:, tt, :], func=AF.Copy, scale=rs),
                          [o_xh, o_stat], [o_ub])
                    for half in range(2):
                        pb = ps[6 + half][:].bitcast(BF16)
                        for j in range(8):
                            c = half * 8 + j
                            T.pe(lambda c=c, j=j, pb=pb: nc.tensor.transpose(
                                out=pb[:, j * 128:(j + 1) * 128], in_=ub[:, c * 128:(c + 1) * 128], identity=ident_b[:]),
                                [o_ub, consts], [o_ps[6 + half]], signal=(j == 7))
                        src = pb.rearrange("p (a b) -> p a b", a=8)
                        if half == 0:
                            T.dve(lambda src=src: nc.vector.tensor_tensor(
                                out=uT[:, 0:8, tt * 128:(tt + 1) * 128], in0=src,
                                in1=gs[:, 0:8].unsqueeze(2).to_broadcast([128, 8, 128]), op=ALU.mult),
                                [o_ps[6], consts], [o_uT])
                        else:
                            T.dve(lambda src=src: nc.vector.tensor_tensor(
                                out=uT[:, 8:16, tt * 128:(tt + 1) * 128], in0=src,
                                in1=gs[:, 8:16].unsqueeze(2).to_broadcast([128, 8, 128]), op=ALU.mult),
                                [o_ps[7], consts], [o_uT])
            norm_transposes(g1s)

            segs = [(0, 128, "halo"), (128, 128, "samp")] if bi == 0 else [(0, 512, "own")]
            zp = a2view(0, [128, 8, 16 + 512], F32)
            tmpA = a2view(17408, [128, 2, 528], F32)
            tmpB = a2view(17408 + 4352, [128, 2, 528], F32)
            dfT = a2view(17408 + 8704, [128, 8, 512], BF16)
            o_zp = T.obj("zp")
            o_tmpA = T.obj("tmpA")
            o_tmpB = T.obj("tmpB")
            o_dfT = T.obj("dfT")
            o_mgT = T.obj("mgT")
            o_gT = T.obj("gT")
            o_fgs = T.obj("fgs")
            o_oT = T.obj("oT")
            bank = 0
            for pc in range(4):
                wv, wo = wload("pq", b_pq[:, pc * 256:(pc + 1) * 256].rearrange("(c p) n -> p c n", p=128), [128, 16, 256])
                for j in range(2):
                    oc = pc * 2 + j
                    b = bank % 4
                    bank += 1
                    for c in range(16):
                        T.pe(lambda c=c, j=j, b=b, wv=wv: nc.tensor.matmul(
                            ps[b][:, 0:N], lhsT=wv[:, c, j * 128:(j + 1) * 128], rhs=uT[:, c, 0:N],
                            start=(c == 0), stop=(c == 15)), [wo, o_uT], [o_ps[b]], signal=(c == 15))
                    if bi == 0:
                        T.act(lambda oc=oc, b=b: nc.scalar.copy(out=zp[:, oc, 16:144], in_=ps[b][:, 0:128]), [o_ps[b]], [o_zp])
                        T.act(lambda oc=oc, b=b: nc.scalar.copy(out=zp[:, oc, 272:400], in_=ps[b][:, 128:256]), [o_ps[b]], [o_zp])
                    else:
                        T.act(lambda oc=oc, b=b: nc.scalar.copy(out=zp[:, oc, 16:528], in_=ps[b][:, 0:512]), [o_ps[b]], [o_zp])
            if bi == 0:
                for c in range(8):
                    T.load(zp[:, c, 256:272], st_pool[:, c * 128:(c + 1) * 128].rearrange("t p -> p t"), o_zp,
                           allow_slow_non_contiguous=True)
                T.dve(lambda: nc.vector.memset(zp[:, :, 0:16], 0.0), [], [o_zp])
            else:
                T.dve(lambda: nc.vector.tensor_copy(out=zp[:, :, 0:16], in_=zprev_p[:]), [o_zprev_p], [o_zp])
            if bi == 0:
                T.dve(lambda: nc.vector.tensor_copy(out=zprev_p[:], in_=zp[:, :, 128:144]), [o_zp], [o_zprev_p])
            else:
                T.dve(lambda: nc.vector.tensor_copy(out=zprev_p[:], in_=zp[:, :, 512:528]), [o_zp], [o_zprev_p])
            def pool_state_out(c0, dst):
                for c in range(8):
                    T.store(dst[:, c * 128:(c + 1) * 128].rearrange("t p -> p t"), zp[:, c, c0:c0 + 15], o_zp,
                            allow_slow_non_contiguous=True)
            if bi == 0:
                pool_state_out(272 + 64 - 15, o_spool)
            if bi == 4:
                pool_state_out(16 + 512 - 15, o_ppool)
            W = 528 if bi > 0 else 400
            for g in range(4):
                zg = zp[:, 2 * g:2 * g + 2, 0:W]
                cur = zg
                o_cur = o_zp
                bufs = [(tmpA, o_tmpA), (tmpB, o_tmpB)]
                for st in range(g + 1):
                    sh = 1 << st
                    dst, o_dst = bufs[st % 2]
                    T.pool(lambda cur=cur, dst=dst, sh=sh: nc.gpsimd.tensor_tensor(
                        out=dst[:, :, sh:W], in0=cur[:, :, sh:W], in1=cur[:, :, 0:W - sh], op=ALU.add),
                        [o_cur], [o_dst])
                    cur, o_cur = dst[:, :, 0:W], o_dst
                w = 2 << g
                if bi == 1:
                    T.dve(lambda cur=cur, g=g: nc.vector.tensor_tensor(
                        out=cur[:, :, 16:32], in0=cur[:, :, 16:32],
                        in1=corrs[:, g * 16:(g + 1) * 16].unsqueeze(1).to_broadcast([128, 2, 16]), op=ALU.mult),
                        [o_cur, consts], [o_cur])
                for (c0, n, kind) in segs:
                    zc0 = 16 if kind != "samp" else 272
                    T.dve(lambda cur=cur, zc0=zc0, n=n, c0=c0, g=g, w=w: nc.vector.scalar_tensor_tensor(
                        out=dfT[:, 2 * g:2 * g + 2, c0:c0 + n], in0=cur[:, :, zc0:zc0 + n], scalar=1.0 / w,
                        in1=zp[:, 2 * g:2 * g + 2, zc0:zc0 + n], op0=ALU.mult, op1=ALU.subtract),
                        [o_cur, o_zp], [o_dfT])
            for pc in range(2):
                wv, wo = wload("pq", b_pq[:, 1024 + pc * 256:1024 + (pc + 1) * 256].rearrange("(c p) n -> p c n", p=128),
                               [128, 16, 256])
                for j in range(2):
                    oc = pc * 2 + j
                    for c in range(16):
                        T.pe(lambda c=c, j=j, oc=oc, wv=wv: nc.tensor.matmul(
                            ps[oc][:, 0:N], lhsT=wv[:, c, j * 128:(j + 1) * 128], rhs=uT[:, c, 0:N],
                            start=(c == 0), stop=(c == 15)), [wo, o_uT], [o_ps[oc]], signal=(c == 15))
            for oc in range(4):
                T.act(lambda oc=oc: nc.scalar.activation(out=sq[:, 0:N], in_=ps[oc][:, 0:N], func=AF.Square),
                      [o_ps[oc]], [o_sq])
                T.pe(lambda oc=oc: nc.tensor.matmul(ps[4][:, 0:N], lhsT=ones_f[:], rhs=sq[:, 0:N],
                                                    start=(oc == 0), stop=(oc == 3)), [o_sq, consts], [o_ps[4]])
            T.act(lambda: nc.scalar.activation(out=rsb[:, 0:N], in_=ps[4][:, 0:N], func=AF.Sqrt, bias=eps_t[:, 0:1],
                                               scale=1.0 / 512), [o_ps[4], consts], [o_rsb])
            T.dve(lambda: nc.vector.reciprocal(out=rsb[:, 0:N], in_=rsb[:, 0:N]), [o_rsb], [o_rsb])
            for oc in range(4):
                T.dve(lambda oc=oc: nc.vector.scalar_tensor_tensor(
                    out=cqn[:, oc, 0:N], in0=ps[oc][:, 0:N], scalar=qgs[:, oc:oc + 1], in1=rsb[:, 0:N],
                    op0=ALU.mult, op1=ALU.mult), [o_ps[oc], consts, o_rsb], [o_cqn])
            for hp in range(4):
                wv, wo = wload("uqp", b_uqp[:, hp * 1024:(hp + 1) * 1024].rearrange("(c p) n -> p c n", p=128),
                               [128, 4, 1024])
                for hh in range(4):
                    h = hp * 4 + hh
                    base = hh * 256
                    b0 = (h % 2) * 3
                    for c in range(4):
                        T.pe(lambda c=c, b0=b0, base=base, wv=wv: nc.tensor.matmul(
                            ps[b0][:, 0:N], lhsT=wv[:, c, base:base + 128], rhs=cqn[:, c, 0:N],
                            start=(c == 0), stop=(c == 3)), [wo, o_cqn], [o_ps[b0]], signal=False)
                    for c in range(4):
                        T.pe(lambda c=c, b0=b0, base=base, wv=wv: nc.tensor.matmul(
                            ps[b0 + 1][:, 0:N], lhsT=wv[:, c, base + 128:base + 256], rhs=cqn[:, c, 0:N],
                            start=(c == 0), stop=(c == 3)), [wo, o_cqn], [o_ps[b0 + 1]], signal=(c == 3))
                    T.act(lambda h=h, b0=b0: nc.scalar.copy(out=qn_v[:, h, 0:N], in_=ps[b0][:, 0:N]),
                          [o_ps[b0]], [o_qn] + ([o_xh] if h == 0 else []))
                    T.dve(lambda b0=b0: nc.vector.tensor_tensor(out=t1[:, 0:N], in0=ps[b0 + 1][:, 0:N],
                                                                 in1=rq[:, 0:N], op=ALU.mult),
                          [o_ps[b0 + 1], o_rq], [o_t1])
                    T.pe(lambda b0=b0: nc.tensor.matmul(ps[b0 + 2][:, 0:N], lhsT=sel_f[:], rhs=t1[:, 0:N],
                                                        start=True, stop=True), [consts, o_t1], [o_ps[b0 + 2]])
                    T.dve(lambda h=h, b0=b0: nc.vector.tensor_copy(out=qr_v[0:64, h, 0:N], in_=ps[b0 + 2][0:64, 0:N]),
                          [o_ps[b0 + 2]], [o_qr] + ([o_xh] if h == 0 else []))
            T.dve(lambda: nc.vector.memset(qr_v[64:65, :, :], NEG), [], [o_qr, o_xh])

            wv, wo = wload("pool", b_pool.rearrange("(g c p) n -> p (g c) n", g=4, c=2), [128, 8, 256])
            for g in range(4):
                for j in range(2):
                    b = (g * 2 + j) % 4
                    for c in range(2):
                        T.pe(lambda g=g, j=j, c=c, b=b, wv=wv: nc.tensor.matmul(
                            ps[b][:, 0:N], lhsT=wv[:, g * 2 + c, j * 128:(j + 1) * 128], rhs=dfT[:, g * 2 + c, 0:N],
                            start=(c == 0), stop=(c == 1)), [wo, o_dfT], [o_ps[b]], signal=(c == 1))
                    T.dve(lambda g=g, j=j, b=b: nc.vector.tensor_scalar(
                        out=yT[:, g * 2 + j, 0:N], in0=ps[b][:, 0:N], scalar1=pscs[:, g * 2 + j:g * 2 + j + 1],
                        scalar2=None, op0=ALU.mult), [o_ps[b], consts], [o_yT])

            a2objs = [o_zp, o_fgs] + [T.obj("%s%d" % (nm, i)) for nm in ("kr", "vr", "rr") for i in range(3)]
            T.soft_barrier(a2objs)
            oT = a2view(0, [128, 16, 512], BF16)
            o_oT = T.obj("oT")
            kr_ = [(a2view(16384 + i * 4096, [128, 2048], BF16), T.obj("kr%d" % i)) for i in range(3)]
            vr_ = [(a2view(16384 + 12288 + i * 4096, [128, 16, 128], BF16), T.obj("vr%d" % i)) for i in range(3)]
            rr_ = [(a2view(16384 + 24576 + i * 4096, [65, 2048], BF16, parts=65), T.obj("rr%d" % i)) for i in range(3)]

            if bi == 0:
                aseg = [(0, 128, list(range(0, 48)), {47: 0}), (128, 64, list(range(64, 97)), {})]
            else:
                last = 48 + 4 * (bi - 1)
                aseg = [(0, 512, list(range(0, last + 4)), {last + i: i for i in range(4)})]
            if bi == 0:
                T.dve(lambda: nc.vector.memset(oT[:, :, 192:256], 0.0), [], [o_oT])
            LOOK = int(os.environ.get("DBG_LOOK", "2"))
            for (c0, n, tiles, diag) in aseg:
                chunks = [tiles[i:i + 16] for i in range(0, len(tiles), 16)]
                allch = [(h, ch) for h in range(NH) for ch in chunks]
                bufs = {}

                def load_chunk(ci):
                    if ci >= len(allch) or ci in bufs:
                        return
                    h, ch = allch[ci]
                    kt0, nt = ch[0], len(ch)
                    kb, o_kb = kr_[ci % 3]
                    vb, o_vb = vr_[ci % 3]
                    rb, o_rb = rr_[ci % 3]
                    a2w = [o_zp, o_tmpA, o_tmpB, o_dfT, o_mgT, o_gT, o_fgs] if ci < 3 else []
                    T.load(kb[:, 0:nt * 128], Ksc[h, :, kt0 * 128:(kt0 + nt) * 128], o_kb, war=a2w)
                    T.load(rb[:, 0:nt * 128], Krsc[:, kt0 * 128:(kt0 + nt) * 128], o_rb, war=a2w)
                    T.load(vb[:, 0:nt, :], Vsc[h, :, kt0:kt0 + nt, :], o_vb, war=a2w)
                    bufs[ci] = (kb, o_kb, vb, o_vb, rb, o_rb)

                tasks = []
                for ci, (h, ch) in enumerate(allch):
                    for j, kt in enumerate(ch):
                        q0, nq = c0, n
                        if kt in diag:
                            q0 = c0 + 128 * diag[kt]
                            nq = n - 128 * diag[kt]
                        tasks.append(dict(ci=ci, h=h, j=j, kt=kt, q0=q0, nq=nq, dg=(kt in diag),
                                          first=(kt == tiles[0]), last=(kt == tiles[-1])))
                load_chunk(0)

                def emit_S(ti):
                    tk = tasks[ti]
                    load_chunk(tk["ci"])
                    kb, o_kb, vb, o_vb, rb, o_rb = bufs[tk["ci"]]
                    sb_ = ti % 3
                    j, h, q0, nq = tk["j"], tk["h"], tk["q0"], tk["nq"]
                    T.pe(lambda: nc.tensor.matmul(ps[sb_][:, 0:nq], lhsT=kb[:, j * 128:(j + 1) * 128],
                                                  rhs=qn_v[:, h, q0:q0 + nq], start=True, stop=False),
                         [o_kb, o_qn], [o_ps[sb_]], signal=False)
                    T.pe(lambda: nc.tensor.matmul(ps[sb_][:, 0:nq], lhsT=rb[0:65, j * 128:(j + 1) * 128],
                                                  rhs=qr_v[0:65, h, q0:q0 + nq], start=False, stop=True),
                         [o_rb, o_qr], [o_ps[sb_]])

                for ti in range(min(LOOK, len(tasks))):
                    emit_S(ti)
                for ti, tk in enumerate(tasks):
                    if ti + LOOK < len(tasks):
                        emit_S(ti + LOOK)
                    if tk["j"] == 0:
                        load_chunk(tk["ci"] + 1)
                    kb, o_kb, vb, o_vb, rb, o_rb = bufs[tk["ci"]]
                    sb_ = ti % 3
                    j, h, q0, nq = tk["j"], tk["h"], tk["q0"], tk["nq"]
                    ob = 3 + 2 * (h % 2)
                    pt, o_pt = PT[ti % 3]
                    T.act(lambda: nc.scalar.activation(out=pt[:, 0:nq], in_=ps[sb_][:, 0:nq], func=AF.Exp,
                                                       scale=SCALE), [o_ps[sb_]], [o_pt])
                    if tk["dg"]:
                        T.dve(lambda: nc.vector.memset(pt[64:128, 0:64], 0.0), [], [o_pt])
                    T.pe(lambda: nc.tensor.matmul(ps[ob][:, q0 - c0:q0 - c0 + nq], lhsT=vb[:, j, :],
                                                  rhs=pt[:, 0:nq], start=tk["first"], stop=tk["last"]),
                         [o_vb, o_pt], [o_ps[ob]], signal=False)
                    T.pe(lambda: nc.tensor.matmul(ps[ob + 1][:, q0 - c0:q0 - c0 + nq], lhsT=ones_b[:],
                                                  rhs=pt[:, 0:nq], start=tk["first"], stop=tk["last"]),
                         [consts, o_pt], [o_ps[ob + 1]])
                    if tk["last"]:
                        T.dve(lambda: nc.vector.tensor_scalar(out=rden[:, 0:n], in0=ps[ob + 1][:, 0:n], scalar1=1e-30,
                                                              scalar2=None, op0=ALU.add), [o_ps[ob + 1]], [o_rden])
                        T.dve(lambda: nc.vector.reciprocal(out=rden[:, 0:n], in_=rden[:, 0:n]), [o_rden], [o_rden])
                        T.dve(lambda: nc.vector.tensor_tensor(out=oT[:, h, c0:c0 + n], in0=ps[ob][:, 0:n],
                                                              in1=rden[:, 0:n], op=ALU.mult), [o_ps[ob], o_rden], [o_oT])

            mgT = a2view(16384, [128, 16, 512], BF16)
            T.soft_barrier(a2objs)
            for oc in range(16):
                r0 = oc * 7168
                wa, woa = wload("mg", b_mg[r0:r0 + 4096, :].rearrange("(c p) n -> p c n", p=128), [128, 32, 128])
                wb_, wob = wload("mg", b_mg[r0 + 4096:r0 + 7168, :].rearrange("(c p) n -> p c n", p=128), [128, 24, 128])
                for c in range(16):
                    T.pe(lambda c=c: nc.tensor.matmul(ps[0][:, 0:N], lhsT=wa[:, c, :], rhs=uT[:, c, 0:N],
                                                      start=(c == 0), stop=(c == 15)), [woa, o_uT], [o_ps[0]], signal=(c == 15))
                for c in range(16):
                    T.pe(lambda c=c: nc.tensor.matmul(ps[1][:, 0:N], lhsT=wa[:, 16 + c, :], rhs=uT[:, c, 0:N],
                                                      start=(c == 0), stop=(c == 15)), [woa, o_uT], [o_ps[1]], signal=(c == 15))
                for c in range(8):
                    T.pe(lambda c=c: nc.tensor.matmul(ps[2][:, 0:N], lhsT=wb_[:, c, :], rhs=yT[:, c, 0:N],
                                                      start=(c == 0), stop=(c == 7)), [wob, o_yT], [o_ps[2]], signal=(c == 7))
                for c in range(16):
                    T.pe(lambda c=c: nc.tensor.matmul(ps[3][:, 0:N], lhsT=wb_[:, 8 + c, :], rhs=oT[:, c, 0:N],
                                                      start=(c == 0), stop=(c == 15)), [wob, o_oT], [o_ps[3]], signal=(c == 15))
                T.act(lambda: nc.scalar.activation(out=sga[:, 0:N], in_=ps[0][:, 0:N], func=AF.Sigmoid), [o_ps[0]], [o_sga])
                T.act(lambda: nc.scalar.activation(out=sgb[:, 0:N], in_=ps[1][:, 0:N], func=AF.Sigmoid), [o_ps[1]], [o_sgb])
                T.dve(lambda: nc.vector.tensor_tensor(out=t1[:, 0:N], in0=ps[2][:, 0:N], in1=sga[:, 0:N], op=ALU.mult),
                      [o_ps[2], o_sga], [o_t1])
                T.dve(lambda: nc.vector.tensor_tensor(out=t2[:, 0:N], in0=ps[3][:, 0:N], in1=sgb[:, 0:N], op=ALU.mult),
                      [o_ps[3], o_sgb], [o_t2])
                T.pool(lambda oc=oc: nc.gpsimd.tensor_tensor(out=mgT[:, oc, 0:N], in0=t1[:, 0:N], in1=t2[:, 0:N], op=ALU.add),
                       [o_t1, o_t2], [o_mgT])

            if bi == 0:
                T.load(xh[:, 0, :], x_win[47 * 128:48 * 128, :], o_xh, war=[o_qn, o_qr])
                T.load(xh[:, 1, :], xs_in, o_xh, war=[o_qn, o_qr])
            else:
                r0 = (48 + (bi - 1) * 4) * 128
                T.load(xh[:, 0:4, :], x_win[r0:r0 + 512, :].rearrange("(t p) n -> p t n", p=128), o_xh,
                       war=[o_qn, o_qr])
            for fb in range(4):
                for half in range(2):
                    wv, wo = wload("out", b_out[half * 1024:(half + 1) * 1024, fb * 512:(fb + 1) * 512]
                                   .rearrange("(c p) n -> p c n", p=128), [128, 8, 512])
                    for tt in range(NT):
                        for c in range(8):
                            kc = half * 8 + c
                            T.pe(lambda tt=tt, c=c, kc=kc, wv=wv: nc.tensor.matmul(
                                ps[tt][:], lhsT=mgT[:, kc, tt * 128:(tt + 1) * 128], rhs=wv[:, c, :],
                                start=(kc == 0), stop=(kc == 15)), [wo, o_mgT], [o_ps[tt]], signal=(c == 7))
                for tt in range(NT):
                    T.dve(lambda tt=tt, fb=fb: nc.vector.tensor_tensor(
                        out=xh[:, tt, fb * 512:(fb + 1) * 512], in0=ps[tt][:], in1=xh[:, tt, fb * 512:(fb + 1) * 512],
                        op=ALU.add), [o_ps[tt], o_xh], [o_xh])

            norm_transposes(g2s)

            T.soft_barrier(a2objs)
            gT = a2view(0, [128, 48, 512], BF16)
            ap_chain, o_apc = aprev_p, o_aprev_p
            if bi == 0:
                for k in range(6):
                    for r in range(2):
                        T.load(aprev_s[:, k * 8:(k + 1) * 8, r],
                               st_conv[r, k * 1024:(k + 1) * 1024].rearrange("(c p) -> p c", p=128), o_aprev_s,
                               allow_slow_non_contiguous=True)
                T.dve(lambda: nc.vector.memset(aprev_p[:], 0.0), [], [o_aprev_p])
            for oc in range(48):
                wv, wo = wload("upp%d" % (oc // 12), b_upp[:, oc * 256:(oc + 1) * 256].rearrange("(c p) n -> p c n", p=128), [128, 16, 256])
                ba = (oc % 2) * 2
                for c in range(16):
                    T.pe(lambda c=c: nc.tensor.matmul(ps[ba][:, 0:N], lhsT=wv[:, c, 0:128], rhs=uT[:, c, 0:N],
                                                      start=(c == 0), stop=(c == 15)), [wo, o_uT], [o_ps[ba]], signal=False)
                for c in range(16):
                    T.pe(lambda c=c: nc.tensor.matmul(ps[ba + 1][:, 0:N], lhsT=wv[:, c, 128:256], rhs=uT[:, c, 0:N],
                                                      start=(c == 0), stop=(c == 15)), [wo, o_uT], [o_ps[ba + 1]],
                         signal=(c == 15))
                for (c0, n, kind) in segs:
                    chain, o_ch = (aprev_s, o_aprev_s) if kind == "samp" else (aprev_p, o_aprev_p)
                    T.act(lambda: nc.scalar.copy(out=t3[:, 0:2], in_=chain[:, oc, :]), [o_ch], [o_t3])
                    T.act(lambda: nc.scalar.copy(out=t3[:, 2:2 + n], in_=ps[ba][:, c0:c0 + n]), [o_ps[ba]], [o_t3])
                    nreal = 64 if kind == "samp" else n
                    T.act(lambda: nc.scalar.copy(out=chain[:, oc, :], in_=t3[:, nreal:nreal + 2]), [o_t3], [o_ch])
                    T.dve(lambda: nc.vector.tensor_scalar(out=t1[:, 0:n], in0=t3[:, 2:2 + n],
                                                          scalar1=cws[:, oc * 3 + 2:oc * 3 + 3], scalar2=cbs[:, oc:oc + 1],
                                                          op0=ALU.mult, op1=ALU.add), [o_t3, consts], [o_t1])
                    T.dve(lambda: nc.vector.scalar_tensor_tensor(out=t2[:, 0:n], in0=t3[:, 1:1 + n],
                                                                 scalar=cws[:, oc * 3 + 1:oc * 3 + 2], in1=t1[:, 0:n],
                                                                 op0=ALU.mult, op1=ALU.add), [o_t3, consts, o_t1], [o_t2])
                    T.dve(lambda: nc.vector.scalar_tensor_tensor(out=t1[:, 0:n], in0=t3[:, 0:n],
                                                                 scalar=cws[:, oc * 3:oc * 3 + 1], in1=t2[:, 0:n],
                                                                 op0=ALU.mult, op1=ALU.add), [o_t3, consts, o_t2], [o_t1])
                    T.act(lambda: nc.scalar.activation(out=t2[:, 0:n], in_=t1[:, 0:n], func=AF.Gelu), [o_t1], [o_t2])
                    T.dve(lambda: nc.vector.tensor_tensor(out=gT[:, oc, c0:c0 + n], in0=ps[ba + 1][:, c0:c0 + n],
                                                          in1=t2[:, 0:n], op=ALU.mult), [o_ps[ba + 1], o_t2], [o_gT])

            def conv_state_out(chain, o_ch, dst):
                for k in range(6):
                    for r in range(2):
                        T.store(dst[r, k * 1024:(k + 1) * 1024].rearrange("(c p) -> p c", p=128),
                                chain[:, k * 8:(k + 1) * 8, r], o_ch, allow_slow_non_contiguous=True)
            if bi == 0:
                conv_state_out(aprev_s, o_aprev_s, o_sconv)
            if bi == 4:
                conv_state_out(aprev_p, o_aprev_p, o_pconv)

            for fb in range(4):
                for k6 in range(6):
                    wv, wo = wload("down%d" % k6, b_down[k6 * 1024:(k6 + 1) * 1024, fb * 512:(fb + 1) * 512]
                                   .rearrange("(c p) n -> p c n", p=128), [128, 8, 512])
                    for tt in range(NT):
                        for c in range(8):
                            kc = k6 * 8 + c
                            T.pe(lambda tt=tt, c=c, kc=kc, wv=wv: nc.tensor.matmul(
                                ps[4 + tt][:], lhsT=gT[:, kc, tt * 128:(tt + 1) * 128], rhs=wv[:, c, :],
                                start=(kc == 0), stop=(kc == 47)), [wo, o_gT], [o_ps[4 + tt]], signal=(c == 7))
                for tt in range(NT):
                    T.dve(lambda tt=tt, fb=fb: nc.vector.tensor_tensor(
                        out=xh[:, tt, fb * 512:(fb + 1) * 512], in0=ps[4 + tt][:], in1=xh[:, tt, fb * 512:(fb + 1) * 512],
                        op=ALU.add), [o_ps[4 + tt], o_xh], [o_xh])

            fgs = a2view(0, [128, D], F32)
            T.dma(SP, fgs, fg_rep, [], [o_fgs, o_gT, o_oT, o_mgT, o_zp], o_fgs)
            for tt in range(NT):
                if bi == 0 and tt == 0:
                    continue
                ss = stat[:, 24 + 2 * tt:25 + 2 * tt]
                rs = stat[:, 25 + 2 * tt:26 + 2 * tt]
                T.act(lambda: nc.scalar.activation(out=ub, in_=xh[:, tt, :], func=AF.Square, accum_out=ss),
                      [o_xh], [o_ub, o_stat])
                rstd_from_ss(ss, 128, D, [o_stat], o_stat, rs)
                T.dve(lambda: nc.vector.scalar_tensor_tensor(out=xh[:, tt, :], in0=xh[:, tt, :], scalar=rs, in1=fgs,
                                                             op0=ALU.mult, op1=ALU.mult), [o_xh, o_stat, o_fgs], [o_xh])
                if bi == 0:
                    T.store(y_s, xh[0:64, tt, :], o_xh)
                else:
                    r0 = (bi - 1) * 512 + tt * 128
                    T.store(y_p[r0:r0 + 128, :], xh[:, tt, :], o_xh)
            if bi == 4:
                T.barrier()
            else:
                T.soft_barrier(a2objs)

        if stage <= 2:
            return nc
        for bi in range(5):
            if stage == 3 and bi > 0:
                break
            run_block(bi)

    return nc


_NC_CACHE = {}


def _rope_tables(pos):
    pos = pos.astype(np.float32)
    inv = (np.float32(10000.0) ** (-(np.arange(32, dtype=np.float32) * np.float32(2.0) / np.float32(64)))).astype(np.float32)
    ang = (pos[:, None] * inv[None, :]).astype(np.float32)
    return np.cos(ang).astype(np.float32), np.sin(ang).astype(np.float32)


def prepare(x_prompt, x_sample, cache_ckv, cache_krope, state_pool, state_conv,
            norm1_g, w_in, pool_w, pool_scale, w_pool_out, q_norm_g, w_uq, kv_norm_g,
            w_uk, w_uv, w_mla_out, w_out, norm2_g, w_up, conv_w, conv_b, w_down, final_g):
    f = lambda a: np.ascontiguousarray(np.asarray(a, dtype=np.float32))
    x_prompt, x_sample = f(x_prompt), f(x_sample)
    w_in0 = f(w_in)[0]
    w_kvr = f(w_in0[:, 1536:2112])
    w_pq = f(w_in0[:, 0:1536])
    gA = w_in0[:, 2112:2112 + 2048]
    gB = w_in0[:, 2112 + 2048:2112 + 4096]
    wpo = f(w_pool_out)[0]
    wmo = f(w_mla_out)[0]
    w_mg = np.concatenate([np.concatenate([gA[:, oc * 128:(oc + 1) * 128], gB[:, oc * 128:(oc + 1) * 128],
                                           wpo[:, oc * 128:(oc + 1) * 128], wmo[:, oc * 128:(oc + 1) * 128]], axis=0)
                           for oc in range(16)], axis=0)
    wuq = f(w_uq)[0].reshape(512, 16, 192)
    w_uqp = np.concatenate([wuq[:, :, 0:128], wuq[:, :, 128:192], wuq[:, :, 160:192], wuq[:, :, 128:160]], axis=2)
    w_uqp = f(w_uqp.reshape(512, 4096))
    wup = f(w_up)[0]
    a_part = wup[:, :DFF].reshape(D, 48, 128)
    b_part = wup[:, DFF:].reshape(D, 48, 128)
    w_upp = f(np.concatenate([a_part, b_part], axis=2).reshape(D, 2 * DFF))
    g = lambda v, n: f(np.asarray(v, np.float32).reshape(n, 128).T)
    shared = {
        "w_kvr": w_kvr, "w_pq": w_pq, "w_mg": f(w_mg), "w_uqp": w_uqp,
        "w_uk": f(f(w_uk)[0].reshape(512, 2048)), "w_uv": f(f(w_uv)[0].reshape(512, 2048)),
        "w_pool": f(f(pool_w)[0].reshape(1024, 256)), "w_out": f(w_out)[0], "w_upp": w_upp, "w_down": f(w_down)[0],
        "g1T": g(f(norm1_g)[0], 16), "g2T": g(f(norm2_g)[0], 16), "qgT": g(f(q_norm_g)[0], 4),
        "pscT": g(f(pool_scale)[0], 8),
        "kvg_rep": f(np.tile(f(kv_norm_g)[0][None, :], (128, 1))),
        "g1_rep": f(np.tile(f(norm1_g)[0][None, :], (128, 1))),
        "fg_rep": f(np.tile(f(final_g)[None, :], (128, 1))),
        "cwT": f(f(conv_w)[0].reshape(3, 48, 128).transpose(2, 1, 0).reshape(128, 144)),
        "cbT": g(f(conv_b)[0], 48),
        "ident": np.eye(128, dtype=np.float32),
        "sel": np.tile(np.eye(64, dtype=np.float32), (2, 2)),
    }
    in_maps = []
    for c in range(8):
        b, qc = c // 4, c % 4
        end = 2048 * (qc + 1)
        start = end - WIN
        xw = np.zeros((WIN, D), np.float32)
        lo = max(start, 0)
        xw[lo - start:] = x_prompt[b, lo:end]
        xs = np.zeros((128, D), np.float32)
        xs[:64] = x_sample[c]
        posw = np.maximum(np.arange(start, end), 0)
        cw_, sw_ = _rope_tables(posw)
        cs_, ss_ = _rope_tables(np.concatenate([4096 + np.arange(64), np.zeros(64)]))
        ropek = np.concatenate([np.concatenate([cw_, sw_], axis=1), np.concatenate([cs_, ss_], axis=1)], axis=0)
        ropeq = np.zeros((5, 128, 512), np.float32)

        def qtab(cos, sin):
            return np.concatenate([cos.T, cos.T, -sin.T, sin.T], axis=0)
        ch, sh = _rope_tables(np.maximum(np.arange(end - 2048 - 128, end - 2048), 0))
        ropeq[0, :, :256] = qtab(np.concatenate([ch, cs_[:64], cs_[64:]], axis=0), np.concatenate([sh, ss_[:64], ss_[64:]], axis=0))
        for j in range(4):
            cj, sj = _rope_tables(np.arange(end - 2048 + 512 * j, end - 2048 + 512 * (j + 1)))
            ropeq[1 + j] = qtab(cj, sj)
        corr = np.ones((128, 4, 16), np.float32)
        if qc == 0:
            for gi, w in enumerate((2, 4, 8, 16)):
                corr[:, gi, :] = (w / np.minimum(np.arange(16) + 1, w)).astype(np.float32)[None, :]
        nullrow = np.zeros((1, NK), np.float32)
        if start < 0:
            nullrow[0, :(-start)] = 1.0
        nullrow[0, NK - 64:] = 1.0
        sp = np.zeros((16, 1024), np.float32)
        sp[1:] = f(state_pool)[0, c]
        m = dict(shared)
        m.update({
            "x_win": xw, "xs": xs, "c_ckv": f(f(cache_ckv)[0, c]), "c_kr": f(f(cache_krope)[0, c]),
            "st_pool": sp, "st_conv": f(f(state_conv)[0, c]), "ropek": f(ropek), "ropeq": ropeq,
            "corr": f(corr.reshape(128, 64)), "nullrow": nullrow,
        })
        in_maps.append(m)
    return in_maps


def kernel(**inputs):
    in_maps = prepare(**inputs)
    if "nc" not in _NC_CACHE:
        _NC_CACHE["nc"] = build_nc()
    nc = _NC_CACHE["nc"]
    res = run_bass_kernel_spmd(nc, in_maps, core_ids=list(range(8)))
    R = res.results
    y_prompt = np.stack([np.concatenate([R[b * 4 + q]["y_p"] for q in range(4)], axis=0) for b in range(2)])
    y_sample = np.stack([R[c]["y_s"] for c in range(8)])
    p_ckv = np.stack([np.concatenate([R[b * 4 + q]["o_pckv"] for q in range(4)], axis=0) for b in range(2)])[None]
    p_kr = np.stack([np.concatenate([R[b * 4 + q]["o_pkr"] for q in range(4)], axis=0) for b in range(2)])[None]
    p_pool = np.stack([R[b * 4 + 3]["o_ppool"] for b in range(2)])[None]
    p_conv = np.stack([R[b * 4 + 3]["o_pconv"] for b in range(2)])[None]
    s_ckv = np.stack([R[c]["o_sckv"] for c in range(8)])[None]
    s_kr = np.stack([R[c]["o_skr"] for c in range(8)])[None]
    s_pool = np.stack([R[c]["o_spool"] for c in range(8)])[None]
    s_conv = np.stack([R[c]["o_sconv"] for c in range(8)])[None]
    outs = (y_prompt, y_sample, p_ckv, p_kr, p_pool, p_conv, s_ckv, s_kr, s_pool, s_conv)
    return tuple(np.ascontiguousarray(o.astype(np.float32)) for o in outs)
```
